# Optimizing a Trainium2 kernel written in Bass

```python
import math
import jax, jax.numpy as jnp
from jax import lax
import numpy as np

D_MODEL = 2048
BATCH = 1
SEQ = 8192
DEPTH = 1

SSM_EXPAND = 2
D_INNER = SSM_EXPAND * D_MODEL
SSM_HEAD_DIM = 64
SSM_HEADS = D_INNER // SSM_HEAD_DIM
SSM_GROUPS = 8
SSM_HEADS_PER_GROUP = SSM_HEADS // SSM_GROUPS
SSM_STATE = 128
SSM_CONV = 4
SSM_CHUNK = 256
D_XBC = D_INNER + 2 * SSM_GROUPS * SSM_STATE
ATTN_HEAD_DIM = 128
ATTN_HEADS = D_MODEL // ATTN_HEAD_DIM
D_ATTN = ATTN_HEADS * ATTN_HEAD_DIM
MOBA_BLOCK = 256
MOBA_TOPK = 3
MOBA_QCHUNK = 32
REL_BUCKETS = 32
REL_MAX_DIST = 128
D_FF = 5632
FFN_CONV = 3
EPS = 1e-6
IN_SPLITS = (D_INNER, D_XBC, SSM_HEADS, D_ATTN, D_ATTN, D_ATTN, 2 * D_MODEL)
D_IN_PROJ = sum(IN_SPLITS)

kernel_name = "hybrid_ssd_moba_convffn"


def _split_points(sizes):
    pts, acc = [], 0
    for s in sizes[:-1]:
        acc += s
        pts.append(acc)
    return pts


def rmsnorm(x, w):
    xf = x.astype(jnp.float32)
    y = xf * lax.rsqrt(jnp.mean(xf * xf, axis=-1, keepdims=True) + EPS)
    return (y * w.astype(jnp.float32)).astype(x.dtype)


def causal_dwconv(x, w, b):
    K = w.shape[0]
    S = x.shape[1]
    xp = jnp.pad(x, ((0, 0), (K - 1, 0), (0, 0)))
    y = b
    for k in range(K):
        y = y + xp[:, k:k + S] * w[k]
    return y


def rel_bucket(dist):
    n = jnp.maximum(dist, 0)
    max_exact = REL_BUCKETS // 2
    nf = jnp.maximum(n, 1).astype(jnp.float32)
    large = max_exact + (jnp.log(nf / max_exact) / math.log(REL_MAX_DIST / max_exact)
                         * (REL_BUCKETS - max_exact)).astype(jnp.int32)
    large = jnp.minimum(large, REL_BUCKETS - 1)
    return jnp.where(n < max_exact, n, large)


def ssd_mixer(z, xbc, dt_raw, conv_w, conv_b, dt_bias, a_log, d_skip, norm_w):
    Bsz, S, _ = xbc.shape
    G, R, P, N, L = SSM_GROUPS, SSM_HEADS_PER_GROUP, SSM_HEAD_DIM, SSM_STATE, SSM_CHUNK
    xbc = jax.nn.silu(causal_dwconv(xbc, conv_w, conv_b))
    xs, Bm, Cm = jnp.split(xbc, [D_INNER, D_INNER + G * N], axis=-1)
    dt = jax.nn.softplus((dt_raw + dt_bias).astype(jnp.float32))
    A = -jnp.exp(a_log.astype(jnp.float32))
    pad = (-S) % L
    Sp = S + pad
    NC = Sp // L

    def chunked(t, tail):
        t = jnp.pad(t, ((0, 0), (0, pad), (0, 0)))
        return t.reshape((Bsz, NC, L) + tail)

    x = chunked(xs, (G, R, P)).astype(jnp.float32)
    Bc = chunked(Bm, (G, N)).astype(jnp.float32)
    Cc = chunked(Cm, (G, N)).astype(jnp.float32)
    dtc = chunked(dt, (G, R))
    X = x * dtc[..., None]
    a_cs = jnp.cumsum(dtc * A.reshape(G, R), axis=2)
    causal = jnp.tril(jnp.ones((L, L), dtype=bool))[None, None, :, :, None, None]
    decay = jnp.exp(jnp.where(causal, a_cs[:, :, :, None] - a_cs[:, :, None, :], -jnp.inf))
    cb = jnp.einsum('bclgn,bcsgn->bclsg', Cc, Bc)
    y_diag = jnp.einsum('bclsg,bclsgr,bcsgrp->bclgrp', cb, decay, X)
    decay_states = jnp.exp(a_cs[:, :, -1:] - a_cs)
    states = jnp.einsum('bclgn,bclgr,bclgrp->bcgrpn', Bc, decay_states, X)
    chunk_decay = jnp.exp(a_cs[:, :, -1])

    def step(h, inp):
        dec, st = inp
        return h * dec[..., None, None] + st, h

    h0 = jnp.zeros((Bsz, G, R, P, N), jnp.float32)
    _, h_prev = lax.scan(step, h0, (jnp.moveaxis(chunk_decay, 1, 0), jnp.moveaxis(states, 1, 0)))
    h_prev = jnp.moveaxis(h_prev, 0, 1)
    y_off = jnp.einsum('bclgn,bcgrpn,bclgr->bclgrp', Cc, h_prev, jnp.exp(a_cs))
    y = y_diag + y_off + x * d_skip.astype(jnp.float32).reshape(G, R)[:, :, None]
    y = y.reshape(Bsz, Sp, D_INNER)[:, :S]
    yg = (y * jax.nn.silu(z.astype(jnp.float32))).reshape(Bsz, S, G, D_INNER // G)
    yg = yg * lax.rsqrt(jnp.mean(yg * yg, axis=-1, keepdims=True) + EPS)
    yg = yg.reshape(Bsz, S, D_INNER) * norm_w.astype(jnp.float32)
    return yg.astype(z.dtype)


def moba_attention(q, k, v, q_norm_w, k_norm_w, rel_table):
    Bsz, S, H, D = q.shape
    BS, QC = MOBA_BLOCK, MOBA_QCHUNK
    q = rmsnorm(q, q_norm_w) * (D ** -0.5)
    k = rmsnorm(k, k_norm_w)
    pad = (-S) % BS
    Sp = S + pad
    NB = Sp // BS
    topk = min(MOBA_TOPK, NB)
    kb = jnp.pad(k, ((0, 0), (0, pad), (0, 0), (0, 0))).reshape(Bsz, NB, BS, H, D).transpose(0, 3, 1, 2, 4)
    vb = jnp.pad(v, ((0, 0), (0, pad), (0, 0), (0, 0))).reshape(Bsz, NB, BS, H, D).transpose(0, 3, 1, 2, 4)
    kbar = jnp.mean(kb, axis=3)
    NQ = S // QC
    qc = jnp.moveaxis(q.transpose(0, 2, 1, 3).reshape(Bsz, H, NQ, QC, D), 2, 0)
    table_h = rel_table.T
    bidx = jnp.arange(Bsz)[:, None, None, None]
    hidx = jnp.arange(H)[None, :, None, None]
    hidx5 = hidx[..., None]

    def one_chunk(args):
        qi, ci = args
        qpos = ci * QC + jnp.arange(QC)
        blk = (ci * QC) // BS
        gate = jnp.einsum('bhqd,bhnd->bhqn', qi, kbar).astype(jnp.float32)
        gate = jnp.where(jnp.arange(NB) < blk, gate, -jnp.inf)
        _, sel = lax.top_k(gate, topk)
        valid = jnp.arange(topk) < blk
        ksel = kb[bidx, hidx, sel]
        vsel = vb[bidx, hidx, sel]
        s_sel = jnp.einsum('bhqd,bhqtkd->bhqtk', qi, ksel).astype(jnp.float32)
        kpos_sel = sel[..., None] * BS + jnp.arange(BS)
        bias_sel = table_h[hidx5, rel_bucket(qpos[:, None, None] - kpos_sel)]
        s_sel = jnp.where(valid[:, None], s_sel + bias_sel, -jnp.inf)
        kown = lax.dynamic_index_in_dim(kb, blk, axis=2, keepdims=False)
        vown = lax.dynamic_index_in_dim(vb, blk, axis=2, keepdims=False)
        s_own = jnp.einsum('bhqd,bhkd->bhqk', qi, kown).astype(jnp.float32)
        dist = qpos[:, None] - (blk * BS + jnp.arange(BS))[None, :]
        bias_own = table_h[:, rel_bucket(dist)]
        s_own = jnp.where(dist >= 0, s_own + bias_own, -jnp.inf)
        logits = jnp.concatenate([s_sel.reshape(Bsz, H, QC, topk * BS), s_own], axis=-1)
        p = jax.nn.softmax(logits, axis=-1).astype(v.dtype)
        p_sel = p[..., :topk * BS].reshape(Bsz, H, QC, topk, BS)
        p_own = p[..., topk * BS:]
        return (jnp.einsum('bhqtk,bhqtkd->bhqd', p_sel, vsel)
                + jnp.einsum('bhqk,bhkd->bhqd', p_own, vown))

    out = lax.map(one_chunk, (qc, jnp.arange(NQ)))
    out = jnp.moveaxis(out, 0, 2).reshape(Bsz, H, S, D).transpose(0, 2, 1, 3)
    return out.reshape(Bsz, S, H * D)


def setup_inputs(seed: int = 0) -> dict:
    key = jax.random.key(seed)
    ks = jax.random.split(key, 24)
    f32 = jnp.float32

    def nrm(k, shape, scale):
        return jax.random.normal(k, shape, f32) * scale

    dt0 = jnp.exp(jax.random.uniform(ks[5], (DEPTH, SSM_HEADS), f32, math.log(1e-3), math.log(1e-1)))
    return {
        "x": nrm(ks[0], (BATCH, SEQ, D_MODEL), 1.0),
        "attn_norm_w": 1.0 + nrm(ks[1], (DEPTH, D_MODEL), 0.02),
        "w_in": nrm(ks[2], (DEPTH, D_MODEL, D_IN_PROJ), D_MODEL ** -0.5),
        "b_gate": nrm(ks[3], (DEPTH, 2 * D_MODEL), 0.02),
        "ssm_conv_w": nrm(ks[4], (DEPTH, SSM_CONV, D_XBC), SSM_CONV ** -0.5),
        "ssm_conv_b": nrm(ks[6], (DEPTH, D_XBC), 0.02),
        "ssm_dt_bias": dt0 + jnp.log(-jnp.expm1(-dt0)),
        "ssm_a_log": jnp.log(jax.random.uniform(ks[7], (DEPTH, SSM_HEADS), f32, 1.0, 16.0)),
        "ssm_d": 1.0 + nrm(ks[8], (DEPTH, SSM_HEADS), 0.1),
        "ssm_norm_w": 1.0 + nrm(ks[9], (DEPTH, D_INNER), 0.02),
        "q_norm_w": 1.0 + nrm(ks[10], (DEPTH, ATTN_HEAD_DIM), 0.02),
        "k_norm_w": 1.0 + nrm(ks[11], (DEPTH, ATTN_HEAD_DIM), 0.02),
        "rel_bias": nrm(ks[12], (REL_BUCKETS, ATTN_HEADS), 0.5),
        "w_ssm_out": nrm(ks[13], (DEPTH, D_INNER, D_MODEL), D_INNER ** -0.5),
        "w_attn_out": nrm(ks[14], (DEPTH, D_ATTN, D_MODEL), D_ATTN ** -0.5),
        "w_out": nrm(ks[15], (DEPTH, D_MODEL, D_MODEL), D_MODEL ** -0.5),
        "ffn_norm_w": 1.0 + nrm(ks[16], (DEPTH, D_MODEL), 0.02),
        "w_up": nrm(ks[17], (DEPTH, D_MODEL, 2 * D_FF), D_MODEL ** -0.5),
        "ffn_conv_w": nrm(ks[18], (DEPTH, FFN_CONV, 2 * D_FF), FFN_CONV ** -0.5),
        "ffn_conv_b": nrm(ks[19], (DEPTH, 2 * D_FF), 0.02),
        "w_down": nrm(ks[20], (DEPTH, D_FF, D_MODEL), D_FF ** -0.5),
    }


def reference(x, attn_norm_w, w_in, b_gate, ssm_conv_w, ssm_conv_b, ssm_dt_bias, ssm_a_log,
              ssm_d, ssm_norm_w, q_norm_w, k_norm_w, rel_bias, w_ssm_out, w_attn_out, w_out,
              ffn_norm_w, w_up, ffn_conv_w, ffn_conv_b, w_down):
    Bsz, S, _ = x.shape
    pts = _split_points(IN_SPLITS)
    for l in range(DEPTH):
        h = rmsnorm(x, attn_norm_w[l])
        proj = h @ w_in[l]
        z, xbc, dt_raw, q, k, v, g = jnp.split(proj, pts, axis=-1)
        y_ssm = ssd_mixer(z, xbc, dt_raw, ssm_conv_w[l], ssm_conv_b[l], ssm_dt_bias[l],
                          ssm_a_log[l], ssm_d[l], ssm_norm_w[l]) @ w_ssm_out[l]
        q = q.reshape(Bsz, S, ATTN_HEADS, ATTN_HEAD_DIM)
        k = k.reshape(Bsz, S, ATTN_HEADS, ATTN_HEAD_DIM)
        v = v.reshape(Bsz, S, ATTN_HEADS, ATTN_HEAD_DIM)
        y_attn = moba_attention(q, k, v, q_norm_w[l], k_norm_w[l], rel_bias) @ w_attn_out[l]
        g_ssm, g_attn = jnp.split(jax.nn.sigmoid(g + b_gate[l]), 2, axis=-1)
        x = x + (g_ssm * y_ssm + g_attn * y_attn) @ w_out[l]
        h = rmsnorm(x, ffn_norm_w[l])
        u = causal_dwconv(h @ w_up[l], ffn_conv_w[l], ffn_conv_b[l])
        u_gate, u_up = jnp.split(u, 2, axis=-1)
        x = x + (jax.nn.silu(u_gate) * u_up) @ w_down[l]
    return x
```

```python
import math
import numpy as np
from contextlib import ExitStack
import concourse.bass as bass
import concourse.mybir as mybir
from concourse.bass_utils import run_bass_kernel_spmd

F32 = mybir.dt.float32
BF16 = mybir.dt.bfloat16
I32 = mybir.dt.int32
AF = mybir.ActivationFunctionType
ALU = mybir.AluOpType
AX = mybir.AxisListType

NCORE = 8
D = 2048
KD = 16
BIG = 30000.0
EPS = 1e-6
ENGS = ("pe", "act", "dve", "pool", "sp")
RING = 8
SAME_ENGINE_SYNC = True


class Buf:
    __slots__ = ("name", "w", "r")

    def __init__(self, name=""):
        self.name = name
        self.w = None
        self.r = {}


class Prog:
    def __init__(self, nc, stack):
        self.nc = nc
        self.stack = stack
        self.ops = {e: [] for e in ENGS}
        self.cnt = {e: 0 for e in ENGS}
        self.seen = {e: {} for e in ENGS}
        self.sems = {}
        for e in ENGS:
            self.sems[e] = stack.enter_context(nc.semaphore("s_" + e))
        self.dma_i = {}
        self.dma_last = {}
        for q in ("sp", "act", "pool"):
            self.dma_i[q] = 0
            for s in range(RING):
                self.sems[("dma", q, s)] = stack.enter_context(nc.semaphore(f"d_{q}_{s}"))

    def _need(self, eng, waits, tok):
        if tok is None:
            return
        key, val = tok
        if key == eng and (not SAME_ENGINE_SYNC or eng in ("pe", "sp")):
            return
        if self.seen[eng].get(key, 0) >= val:
            return
        waits[key] = max(waits.get(key, 0), val)

    def _deps(self, eng, reads, writes):
        waits = {}
        for b in reads:
            self._need(eng, waits, b.w)
        for b in writes:
            self._need(eng, waits, b.w)
            for k, v in b.r.items():
                self._need(eng, waits, (k, v))
        for k, v in waits.items():
            self.seen[eng][k] = v
        return waits

    def _mark(self, tok, reads, writes):
        key, val = tok
        for b in reads:
            if b.r.get(key, 0) < val:
                b.r[key] = val
        for b in writes:
            b.w = tok
            b.r = {}

    def op(self, eng, emit, reads=(), writes=(), inc=True):
        waits = self._deps(eng, reads, writes)
        if inc:
            self.cnt[eng] += 1
            tok = (eng, self.cnt[eng])
        else:
            tok = (eng, self.cnt[eng] + 1)
        self._mark(tok, reads, writes)
        self.ops[eng].append((list(waits.items()), emit, (eng, 1) if inc else None))
        return tok

    def dma(self, q, out, in_, reads=(), writes=(), emit=None, **kw):
        i = self.dma_i[q]
        self.dma_i[q] = i + 1
        slot = i % RING
        key = ("dma", q, slot)
        waits = self._deps(q, reads, writes)
        if i >= RING:
            prev = 16 * (i // RING)
            if self.seen[q].get(key, 0) < prev:
                waits[key] = max(waits.get(key, 0), prev)
                self.seen[q][key] = prev
        tok = (key, 16 * (i // RING + 1))
        self.dma_last[key] = tok[1]
        self._mark(tok, reads, writes)
        if emit is None:
            def emit(e, out=out, in_=in_, kw=kw):
                return e.dma_start(out=out, in_=in_, **kw)
        self.ops[q].append((list(waits.items()), emit, (key, 16)))
        return tok

    def coll(self, emit, reads=(), writes=()):
        key = ("cc", len([k for k in self.sems if isinstance(k, tuple) and k[0] == "cc"]))
        self.sems[key] = self.stack.enter_context(self.nc.semaphore(f"cc_{key[1]}"))
        if NOCC:
            scr = self.cc_scratch[:, key[1]:key[1] + 1]
            emit = lambda e, scr=scr: e.memset(scr, 0.0)
        waits = self._deps("pool", reads, writes)
        tok = (key, 1)
        self.dma_last[key] = 1
        self._mark(tok, reads, writes)
        self.ops["pool"].append((list(waits.items()), emit, (key, 1)))
        return tok

    def barrier(self):
        toks = [(e, self.cnt[e]) for e in ENGS if self.cnt[e] > 0]
        toks += [(k, v) for k, v in self.dma_last.items()]
        for e in ENGS:
            waits = {}
            for t in toks:
                self._need(e, waits, t) if t[0] != e else None
            for k, v in waits.items():
                self.seen[e][k] = v
            if waits:
                self.ops[e].append((list(waits.items()), None, None))

    def emit_all(self):
        nc = self.nc
        with nc.Block() as block:
            def run(e, name):
                for waits, emit, inc in self.ops[name]:
                    for k, v in waits:
                        e.wait_ge(self.sems[k], v)
                    if emit is None:
                        continue
                    ins = emit(e)
                    if inc is not None:
                        ins.then_inc(self.sems[inc[0]], inc[1])

            @block.tensor
            def _(e):
                run(e, "pe")

            @block.scalar
            def _(e):
                run(e, "act")

            @block.vector
            def _(e):
                run(e, "dve")

            @block.gpsimd
            def _(e):
                run(e, "pool")

            @block.sync
            def _(e):
                run(e, "sp")


class Arena:
    def __init__(self, t, words):
        self.t = t
        self.words = words
        self.off = 0

    def alloc(self, shape, dtype, parts=128):
        n = int(np.prod(shape))
        sz = 2 if dtype == BF16 else 4
        words = (n * sz + 3) // 4
        words = (words + 7) // 8 * 8
        assert self.off + words <= self.words, ("SBUF arena overflow", self.off, words, self.words)
        ap = self.t[0:parts, self.off:self.off + words]
        self.off += words
        if dtype != F32:
            ap = ap.bitcast(dtype)
        ap = ap[:, 0:n]
        if len(shape) == 2:
            ap = ap.rearrange("p (a b) -> p a b", b=shape[1])
        elif len(shape) == 3:
            ap = ap.rearrange("p (a b c) -> p a b c", b=shape[1], c=shape[2])
        return ap

    def mark(self):
        return self.off

    def release(self, m):
        self.off = m


def bc(ap, shape):
    return ap.broadcast_to(list(shape))


def build(S, dbg=False):
    NT = S // 512
    NB = S // 256
    NTOK = S // NCORE
    TPS = NTOK // 512
    assert NTOK % 512 == 0 and NB <= 32
    NT4 = NTOK + 2
    n4 = -(-NT4 // 512)
    tw4 = -(-NT4 // n4)
    tw4 += tw4 % 2
    T4 = [(a0, min(a0 + tw4, NT4)) for a0 in range(0, NT4, tw4)]

    nc = bass.Bass("TRN2", target_bir_lowering=False)

    def din(name, shape, dt=F32):
        return nc.dram_tensor(name, list(shape), dt, kind="ExternalInput").ap()

    xT = din("xT", [D, S])
    x4 = din("x4", [D, NT4])
    w1 = din("w1", [128, KD, 1280])
    w2 = din("w2", [128, KD, 776])
    sm = din("sm", [128, 512])
    bcst = din("bcst", [128, 1152])
    cst = din("cst", [128, 1792])
    esel = din("esel", [32, 32 * 128])
    cmask = din("cmask", [128, 5 * 512])
    tbr = din("tbr", [128, 2 * 5 * 512])
    fcw = din("fcw", [128, 88 * 4])
    gidx = din("gidx", [128, 96], I32)
    L1 = (lambda n: 1) if STOP != 0 else (lambda n: n)
    wg = din("wg", [L1(32), 128, KD * 128])
    wso = din("wso", [L1(16), 128, 32 * 128])
    wao = din("wao", [L1(16), 128, KD * 128])
    wo = din("wo", [L1(16), 128, KD * 128])
    wup = din("wup", [L1(88), 128, KD * 128])
    wdn = din("wdn", [L1(16), 128, 44 * 128])
    outT = nc.dram_tensor("outT", [D, NTOK], F32, kind="ExternalOutput").ap()

    ssd_in = nc.dram_tensor("ssd_in", [NCORE, 512, NTOK], BF16, kind="Internal").ap()
    ssd_out = nc.dram_tensor("ssd_out", [NCORE, NCORE * 512, NTOK], BF16, kind="Internal").ap()
    att_in = nc.dram_tensor("att_in", [NCORE, 256, NTOK], BF16, kind="Internal").ap()
    att_out = nc.dram_tensor("att_out", [NCORE, NCORE * 256, NTOK], BF16, kind="Internal").ap()
    xmid = nc.dram_tensor("xmid", [D, NT4], F32, kind="Internal").ap()
    ssd_tin = nc.dram_tensor("ssd_tin", [NCORE * 512, 2], BF16, kind="Internal").ap()
    ssd_tout = nc.dram_tensor("ssd_tout", [NCORE * NCORE * 512, 2], BF16, kind="Internal").ap()
    att_tin = nc.dram_tensor("att_tin", [NCORE * 256, 2], BF16, kind="Internal").ap()
    att_tout = nc.dram_tensor("att_tout", [NCORE * NCORE * 256, 2], BF16, kind="Internal").ap()
    if dbg:
        dbg_ssd = nc.dram_tensor("dbg_ssd", [NCORE, 512, NTOK], BF16, kind="ExternalOutput").ap()
        dbg_att = nc.dram_tensor("dbg_att", [NCORE, 256, NTOK], BF16, kind="ExternalOutput").ap()
        dbg_xm = nc.dram_tensor("dbg_xm", [D, NT4], F32, kind="ExternalOutput").ap()

    with ExitStack() as st:
        P = Prog(nc, st)
        AW = 51 * 1024
        arena_t = st.enter_context(nc.sbuf_tensor("arena", [128, AW], F32))
        A = Arena(arena_t, AW)
        PS = [st.enter_context(nc.psum_tensor(f"ps{i}", [128, 512], F32))[:, :] for i in range(8)]
        PSB = [Buf(f"ps{i}") for i in range(8)]

        def mm(out, lhsT, rhs, start, stop, reads, writes, inc=None):
            inc = True
            P.op("pe", lambda e: e.matmul(out, lhsT=lhsT, rhs=rhs, start=start, stop=stop),
                 reads=reads, writes=writes, inc=inc)

        def tr(out, in_, ident, reads, writes):
            P.op("pe", lambda e: e.transpose(out=out, in_=in_, identity=ident), reads=reads, writes=writes)

        def act(out, in_, func, reads, writes, bias=0.0, scale=1.0, accum=None):
            if accum is None:
                P.op("act", lambda e: e.activation(out=out, in_=in_, func=func, bias=bias, scale=scale),
                     reads=reads, writes=writes)
            else:
                P.op("act", lambda e: e.activation(out=out, in_=in_, func=func, bias=bias, scale=scale,
                                                   accum_out=accum), reads=reads, writes=writes)

        def tt(eng, out, in0, in1, op, reads, writes):
            P.op(eng, lambda e: e.tensor_tensor(out=out, in0=in0, in1=in1, op=op), reads=reads, writes=writes)

        def ts(eng, out, in0, s1, s2, op0, op1, reads, writes):
            if s2 is None:
                P.op(eng, lambda e: e.tensor_scalar(out=out, in0=in0, scalar1=s1, scalar2=None, op0=op0),
                     reads=reads, writes=writes)
            else:
                P.op(eng, lambda e: e.tensor_scalar(out=out, in0=in0, scalar1=s1, scalar2=s2, op0=op0, op1=op1),
                     reads=reads, writes=writes)

        def stt(eng, out, in0, sc, in1, op0, op1, reads, writes):
            P.op(eng, lambda e: e.scalar_tensor_tensor(out=out, in0=in0, scalar=sc, in1=in1, op0=op0, op1=op1),
                 reads=reads, writes=writes)

        def cp(eng, out, in_, reads, writes):
            if eng == "act":
                P.op("act", lambda e: e.copy(out=out, in_=in_), reads=reads, writes=writes)
            else:
                P.op(eng, lambda e: e.tensor_copy(out=out, in_=in_), reads=reads, writes=writes)

        def recip(out, in_, reads, writes):
            P.op("dve", lambda e: e.reciprocal(out=out, in_=in_), reads=reads, writes=writes)

        def rsqrt_chain(out, in_, mul, reads, writes, tmpb):
            ts("dve", out, in_, mul, EPS, ALU.mult, ALU.add, reads, writes)
            act(out, out, AF.Sqrt, writes, writes)
            recip(out, out, writes, writes)

        smt = A.alloc([512], F32)
        b_sm = Buf("sm")
        P.dma("sp", smt, sm, writes=[b_sm])
        bct = A.alloc([1152], F32)
        b_bc = Buf("bc")
        P.dma("sp", bct, bcst, writes=[b_bc])
        cstf = A.alloc([1792], F32)
        b_cst = Buf("cst")
        P.dma("sp", cstf, cst, writes=[b_cst])
        cstb = A.alloc([1792], BF16)
        P.dma("pool", cstb, cst, writes=[b_cst])
        anw = smt[:, 0:16]
        fnw = smt[:, 16:32]
        qkw = smt[:, 32:34]
        b31 = smt[:, 34:36]
        flag = smt[:, 36:37]
        bgate = smt[:, 40:72]
        cvw = smt[:, 72:96].rearrange("p (a b) -> p a b", b=4)
        cvb = smt[:, 96:102]
        fcb = smt[:, 128:216]
        dtb4 = bct[:, 0:32]
        alog4 = bct[:, 32:64]
        dvec = bct[:, 64:72]
        normw = bct[:, 128:640]
        ident_f = cstf[:, 0:128]
        ones_f = cstf[:, 128:256]
        tri_f = cstf[:, 256:384]
        causal3 = cstf[:, 512:896].rearrange("p (a b) -> p a b", b=128)
        vn = cstf[:, 896:960]
        pm = cstf[:, 960:1024]
        ident_b = cstb[:, 0:128]
        ones_b = cstb[:, 128:256]
        tri_b = cstb[:, 256:384]
        strict_b = cstb[:, 384:512]
        P.cc_scratch = A.alloc([32], F32)
        dtraw = A.alloc([S // 128, 8], F32)
        b_dtraw = [Buf(f"dtraw{t}") for t in range(NT)]
        A4 = A.alloc([32], F32)
        b_A4 = Buf("A4")
        act(A4, alog4, AF.Exp, [b_bc], [b_A4])
        ts("dve", A4, A4, -1.0, None, ALU.mult, None, [b_A4], [b_A4])
        qkws = A.alloc([2], F32)
        b_qkws = Buf("qkws")
        ts("dve", qkws[:, 0:1], qkw[:, 0:1], 128.0 ** -0.5, None, ALU.mult, None, [b_sm], [b_qkws])
        cp("dve", qkws[:, 1:2], qkw[:, 1:2], [b_sm], [b_qkws])
        base_mark = A.mark()

        xv = xT.rearrange("(k p) s -> p k s", p=128)

        def load_norm(t, xb, b_xb, h, b_h, sq, b_sq, rstd, b_rstd, ps_i):
            for g in range(4):
                P.dma("pool", xb[:, 4 * g:4 * g + 4, :], xv[:, 4 * g:4 * g + 4, t * 512:(t + 1) * 512],
                      writes=[b_xb[g]])
            for g in range(4):
                s = sq[g % 2]
                act(s, xb[:, 4 * g:4 * g + 4, :], AF.Square, [b_xb[g]], [b_sq[g % 2]])
                for k in range(4):
                    mm(PS[ps_i], ones_b, s[:, k, :], g == 0 and k == 0, g == 3 and k == 3,
                       [b_cst, b_sq[g % 2]], [PSB[ps_i]])
            rsqrt_chain(rstd, PS[ps_i], 1.0 / D, [PSB[ps_i]], [b_rstd], None)
            for kd in range(KD):
                stt("dve", h[:, kd, :], xb[:, kd, :], anw[:, kd:kd + 1], rstd,
                    ALU.mult, ALU.mult, [b_xb[kd // 4], b_sm, b_rstd], [b_h])

        QT = A.alloc([2, S], BF16)
        KT = A.alloc([2, S], BF16)
        VT = A.alloc([S // 128, 256], BF16)
        b_QT = [[Buf() for _ in range(NT)] for _ in range(2)]
        b_KT = [[Buf() for _ in range(NT)] for _ in range(2)]
        b_VT = [Buf() for _ in range(NT)]
        kbar_f = A.alloc([2, 32], F32)
        kbar_b = A.alloc([2, 32], BF16)
        b_kbar = Buf("kbar")
        p23_mark = A.mark()
        W2 = A.alloc([KD, 776], BF16)
        b_W2 = Buf("W2")
        for g in range(4):
            P.dma("pool", W2[:, 4 * g:4 * g + 4, :], w2[:, 4 * g:4 * g + 4, :], writes=[b_W2])
        xb = A.alloc([KD, 512], BF16)
        b_xb = [Buf() for _ in range(4)]
        hb2 = [A.alloc([KD, 512], BF16) for _ in range(2)]
        b_h2 = [Buf("h0"), Buf("h1")]
        sq = [A.alloc([4, 512], BF16) for _ in range(2)]
        b_sq = [Buf(), Buf()]
        rstd = A.alloc([512], F32)
        b_rstd = Buf()
        sqq = [A.alloc([512], BF16) for _ in range(2)]
        b_sqq = [Buf(), Buf()]
        rq = [A.alloc([512], F32) for _ in range(2)]
        b_rq = [Buf(), Buf()]

        P.op("dve", lambda e: e.memset(kbar_f, 0.0), writes=[b_kbar])
        if STOP == 20:
            P.barrier()
            P.emit_all()
            return nc
        NT2 = 0 if SKIP2 else NT
        if NT2:
            load_norm(0, xb, b_xb, hb2[0], b_h2[0], sq, b_sq, rstd, b_rstd, 7)
        for t in range(NT2):
            hb, b_h = hb2[t % 2], b_h2[t % 2]
            if STOP == 21:
                P.barrier()
                P.emit_all()
                return nc
            for m in range(4):
                pi = m % 4
                for kd in range(KD):
                    mm(PS[pi], W2[:, kd, m * 128:(m + 1) * 128], hb[:, kd, :], kd == 0, kd == KD - 1,
                       [b_W2, b_h], [PSB[pi]])
                if STOP == 23:
                    P.barrier(); P.emit_all(); return nc
                r = m % 2
                act(sqq[r], PS[pi], AF.Square, [PSB[pi]], [b_sqq[r]])
                mm(PS[4 + r], ones_b, sqq[r], True, True, [b_cst, b_sqq[r]], [PSB[4 + r]])
                rsqrt_chain(rq[r], PS[4 + r], 1.0 / 128, [PSB[4 + r]], [b_rq[r]], None)
                if STOP == 24:
                    P.barrier(); P.emit_all(); return nc
                hh = m % 2
                if m < 2:
                    dst, bd, sc = QT[:, hh, t * 512:(t + 1) * 512], b_QT[hh][t], qkws[:, 0:1]
                else:
                    dst, bd, sc = KT[:, hh, t * 512:(t + 1) * 512], b_KT[hh][t], qkws[:, 1:2]
                stt("dve", dst, PS[pi], sc, rq[r], ALU.mult, ALU.mult, [PSB[pi], b_rq[r], b_qkws], [bd])
                if STOP == 25:
                    P.barrier(); P.emit_all(); return nc
                if m >= 2:
                    P.op("dve", lambda e, hh=hh, t=t: e.reduce_sum(
                        out=kbar_f[:, hh, 2 * t:2 * t + 2],
                        in_=KT[:, hh, t * 512:(t + 1) * 512].rearrange("p (a b) -> p a b", b=256), axis=AX.X),
                        reads=[b_KT[hh][t]], writes=[b_kbar])
            if STOP == 26:
                P.barrier(); P.emit_all(); return nc
            if t + 1 < NT2:
                load_norm(t + 1, xb, b_xb, hb2[(t + 1) % 2], b_h2[(t + 1) % 2], sq, b_sq, rstd, b_rstd, 7)
            for sub in range(4):
                for kd in range(KD):
                    mm(PS[6][:, 0:264], hb[:, kd, sub * 128:(sub + 1) * 128], W2[:, kd, 512:776], kd == 0,
                       kd == KD - 1, [b_W2, b_h], [PSB[6]])
                if STOP == 27:
                    P.barrier(); P.emit_all(); return nc
                cp("act", VT[:, t * 4 + sub, :], PS[6][:, 0:256], [PSB[6]], [b_VT[t]])
                if STOP == 28:
                    P.barrier(); P.emit_all(); return nc
                cp("act", dtraw[:, t * 4 + sub, :], PS[6][:, 256:264], [PSB[6]], [b_dtraw[t]])
            if STOP == 22:
                P.barrier()
                P.emit_all()
                return nc
        cp("dve", kbar_b, kbar_f, [b_kbar], [b_kbar])
        P.barrier()
        A.release(p23_mark)
        if STOP == 2:
            P.emit_all()
            return nc

        cm = A.alloc([5, 512], F32)
        b_cm = Buf("cm")
        P.dma("sp", cm, cmask.rearrange("p (a b) -> p a b", b=512), writes=[b_cm])
        tbraw = A.alloc([2, 5, 512], F32)
        b_tbraw = Buf()
        P.dma("sp", tbraw, tbr.rearrange("p (h a b) -> p h a b", a=5, b=512), writes=[b_tbraw])
        TBf = A.alloc([2, 5, 512], BF16)
        b_TBf = Buf("TBf")
        for hh in range(2):
            stt("dve", TBf[:, hh], tbraw[:, hh], b31[:, hh:hh + 1], cm, ALU.subtract, ALU.add,
                [b_tbraw, b_cm, b_sm], [b_TBf])
        eselb = A.alloc([32, 128], BF16, parts=32)
        b_esel = Buf("esel")
        P.dma("pool", eselb, esel.rearrange("p (a b) -> p a b", b=128), writes=[b_esel])
        gm = A.alloc([4, 32], F32)
        b_gm = Buf()
        top8 = A.alloc([4, 8], F32)
        b_top8 = Buf()
        negm = A.alloc([4, 32], F32)
        b_negm = Buf()
        b_gmL = [Buf() for _ in range(4)]
        b_top8L = [Buf() for _ in range(4)]
        b_negmL = [Buf() for _ in range(4)]
        negT = [A.alloc([512], BF16, parts=32) for _ in range(2)]
        b_negT = [Buf(), Buf()]
        pT = [A.alloc([512], BF16) for _ in range(3)]
        b_pT = [Buf() for _ in range(3)]
        rden = A.alloc([512], F32)
        b_rden = Buf()
        ao = [A.alloc([512], BF16) for _ in range(2)]
        b_ao = [Buf(), Buf()]
        b_attin = [Buf() for _ in range(NCORE)]
        b_attout = [Buf() for _ in range(NCORE)]
        b_atttin = Buf()
        b_atttout = Buf()
        b_ssdtin = Buf()
        b_ssdtout = Buf()
        srot = [0]
        prot = [0]

        def sel(i, hh, nb):
            for s in range(4):
                mm(PS[3][:, s * 32:(s + 1) * 32], QT[:, hh, i * 512 + s * 128:i * 512 + (s + 1) * 128],
                   kbar_b[:, hh, :], True, True, [b_QT[hh][i], b_kbar], [PSB[3]])
            blks = [2 * i + s // 2 for s in range(4)]
            for s in range(4):
                tt("dve", gm[:, s, :], PS[3][:, s * 32:(s + 1) * 32], vn[:, 32 - blks[s]:64 - blks[s]], ALU.add,
                   [PSB[3], b_cst], [b_gmL[s]])
            for s in range(4):
                P.op("dve", lambda e, s=s: e.max(out=top8[:, s, :], in_=gm[:, s, :]), reads=[b_gmL[s]],
                     writes=[b_top8L[s]])
            for s in range(4):
                ts("dve", negm[:, s, :], gm[:, s, :], top8[:, s, 2:3], None, ALU.is_ge, None,
                   [b_gmL[s], b_top8L[s]], [b_negmL[s]])
            for s in range(4):
                ts("dve", negm[:, s, :], negm[:, s, :], -1.0, BIG, ALU.add, ALU.mult, [b_negmL[s]], [b_negmL[s]])
            for s in range(4):
                tt("dve", negm[:, s, :], negm[:, s, :], pm[:, 32 - blks[s]:64 - blks[s]], ALU.mult,
                   [b_negmL[s], b_cst], [b_negmL[s]])
            sb_i = srot[0] % 3
            srot[0] += 1
            for s in range(4):
                tr(PS[sb_i][0:32, s * 128:(s + 1) * 128], negm[:, s, :], ident_f, [b_negmL[s], b_cst], [PSB[sb_i]])
            cp("act", negT[nb], PS[sb_i][0:32, :], [PSB[sb_i]], [b_negT[nb]])

        def score(i, hh, nb, j):
            sb_i = srot[0] % 3
            srot[0] += 1
            near = j >= 4 * i - 1
            mm(PS[sb_i], KT[:, hh, j * 128:(j + 1) * 128], QT[:, hh, i * 512:(i + 1) * 512], True, False,
               [b_KT[hh][j // 4], b_QT[hh][i]], [PSB[sb_i]])
            mm(PS[sb_i], eselb[:, j // 2, :], negT[nb], False, not near, [b_esel, b_negT[nb]], [PSB[sb_i]])
            if near:
                rt = j - (4 * i - 1)
                mm(PS[sb_i], ident_b, TBf[:, hh, rt, :], False, True, [b_cst, b_TBf], [PSB[sb_i]])
            return sb_i

        def pv(hh, j, nk, sb_i, po, pd):
            pr = prot[0] % 3
            prot[0] += 1
            act(pT[pr], PS[sb_i], AF.Exp, [PSB[sb_i]], [b_pT[pr]])
            mm(PS[po], VT[:, j, hh * 128:(hh + 1) * 128], pT[pr], j == 0, j == nk - 1,
               [b_VT[j // 4], b_pT[pr]], [PSB[po]])
            mm(PS[pd], ones_b, pT[pr], j == 0, j == nk - 1, [b_cst, b_pT[pr]], [PSB[pd]])

        iters = [(i, hh) for i in range(0 if SKIP3 else NT) for hh in range(2)]
        if iters:
            sel(iters[0][0], iters[0][1], 0)
        for it, (i, hh) in enumerate(iters):
            nb = it % 2
            if it + 1 < len(iters):
                sel(iters[it + 1][0], iters[it + 1][1], (it + 1) % 2)
            po, pd = 4 + (it % 2) * 2, 5 + (it % 2) * 2
            nk = 4 * i + 4
            prev = score(i, hh, nb, 0)
            for j in range(nk):
                nxt = score(i, hh, nb, j + 1) if j + 1 < nk else None
                pv(hh, j, nk, prev, po, pd)
                prev = nxt
            recip(rden, PS[pd], [PSB[pd]], [b_rden])
            ar = it % 2
            tt("dve", ao[ar], PS[po], rden, ALU.mult, [PSB[po], b_rden], [b_ao[ar]])
            seg = (i * 512) // NTOK
            c0 = i * 512 - seg * NTOK
            P.dma("sp", att_in[seg, hh * 128:(hh + 1) * 128, c0:c0 + 512], ao[ar], reads=[b_ao[ar]],
                  writes=[b_attin[seg]])
            if (i + 1) % TPS == 0:
                P.dma("sp", att_tin[seg * 256 + hh * 128:seg * 256 + (hh + 1) * 128, :], ao[ar][:, 510:512],
                      reads=[b_ao[ar]], writes=[b_atttin])
            if hh == 1 and (i + 1) % TPS == 0:
                P.coll(lambda e, seg=seg: e.collective_compute(
                    "AllGather", ALU.bypass, replica_groups=[list(range(NCORE))],
                    ins=[att_in[seg]], outs=[att_out[seg]]), reads=[b_attin[seg]], writes=[b_attout[seg]])
        P.coll(lambda e: e.collective_compute(
            "AllGather", ALU.bypass, replica_groups=[list(range(NCORE))],
            ins=[att_tin], outs=[att_tout]), reads=[b_atttin], writes=[b_atttout])
        P.barrier()
        A.release(base_mark)
        if STOP == 3:
            for sg in range(NCORE):
                P.dma("sp", dbg_att[sg], att_in[sg], reads=[b_attin[sg]])
            P.barrier()
            P.emit_all()
            return nc

        W1 = A.alloc([KD, 1280], BF16)
        b_W1 = Buf("W1")
        for g in range(4):
            P.dma("pool", W1[:, 4 * g:4 * g + 4, :], w1[:, 4 * g:4 * g + 4, :], writes=[b_W1])
        xb = A.alloc([KD, 512], BF16)
        b_xb = [Buf() for _ in range(4)]
        hb2 = [A.alloc([KD, 512], BF16) for _ in range(2)]
        b_h2 = [Buf("h0"), Buf("h1")]
        sq = [A.alloc([4, 512], BF16) for _ in range(2)]
        b_sq = [Buf(), Buf()]
        rstd = A.alloc([512], F32)
        b_rstd = Buf()
        xraw1 = A.alloc([6, 515], F32)
        b_xraw1 = Buf()
        cacc = [A.alloc([512], F32) for _ in range(2)]
        b_cacc = [Buf(), Buf()]
        xc = A.alloc([6, 512], BF16)
        b_xc = [Buf() for _ in range(6)]
        zs = A.alloc([4, 512], BF16)
        b_zs = [Buf() for _ in range(4)]
        xs_tok = A.alloc([4, 512], BF16)
        b_xs = [Buf() for _ in range(4)]
        Xt = A.alloc([4, 512], BF16)
        b_X = [Buf() for _ in range(4)]
        Xd = A.alloc([4, 512], BF16)
        b_Xd = [Buf() for _ in range(4)]
        Btok = A.alloc([4, 128], BF16)
        b_Btok = [Buf() for _ in range(4)]
        dt_t = A.alloc([32], F32)
        b_dt = Buf()
        dtA = A.alloc([32], F32)
        b_dtA = Buf()
        dAh = A.alloc([4, 16], BF16)
        b_dAh = Buf()
        dAtmp = A.alloc([32], F32)
        b_dAtmp = Buf()
        acs48 = A.alloc([48], F32)
        b_acs48 = Buf()
        LA = [A.alloc([8, 128], BF16) for _ in range(4)]
        b_LA = [Buf() for _ in range(4)]
        LB = [A.alloc([8, 128], BF16) for _ in range(2)]
        b_LB = [Buf(), Buf()]
        cbm = [A.alloc([3, 128], BF16) for _ in range(2)]
        b_cbm = [Buf(), Buf()]
        dec = [A.alloc([3, 128], BF16) for _ in range(2)]
        b_dec = [Buf(), Buf()]
        MT = [A.alloc([3, 128], BF16) for _ in range(2)]
        b_MT = [Buf(), Buf()]
        acs = A.alloc([3, 8], F32)
        b_acs = Buf()
        ea = A.alloc([2, 8], F32)
        b_ea = Buf()
        dsx = A.alloc([2, 8], F32)
        b_dsx = Buf()
        cdx = A.alloc([8], F32)
        b_cdx = Buf()
        t1 = [A.alloc([512], F32) for _ in range(2)]
        b_t1 = [Buf(), Buf()]
        yb = [A.alloc([512], F32) for _ in range(2)]
        b_y = [Buf(), Buf()]
        ssq = A.alloc([4], F32)
        b_ssq = Buf()
        b_ssqL = [Buf(), Buf()]
        yg = [A.alloc([512], BF16) for _ in range(2)]
        b_yg = [Buf(), Buf()]
        ygT = [A.alloc([4, 512], BF16) for _ in range(2)]
        b_ygT = [Buf(), Buf()]
        hT = A.alloc([512], F32)
        b_hT = Buf()
        hTb = A.alloc([512], BF16)
        b_hTb = Buf()
        Dt = A.alloc([8, 64], F32)
        b_Dt = Buf()
        cp("dve", Dt, bc(dvec.unsqueeze(2), [128, 8, 64]), [b_bc], [b_Dt])
        P.op("dve", lambda e: e.memset(hT, 0.0), writes=[b_hT])
        P.op("dve", lambda e: e.memset(hTb, 0.0), writes=[b_hTb])
        P.op("dve", lambda e: e.memset(xraw1[:, :, 512:515], 0.0), writes=[b_xraw1])
        b_ssdin = [Buf() for _ in range(NCORE)]
        b_ssdout = [Buf() for _ in range(NCORE)]
        PSBF = [p.bitcast(BF16) for p in PS]

        NT1 = 0 if SKIP1 else NT
        if NT1:
            load_norm(0, xb, b_xb, hb2[0], b_h2[0], sq, b_sq, rstd, b_rstd, 7)
        for t in range(NT1):
            hb, b_h = hb2[t % 2], b_h2[t % 2]
            xr, bxr = xraw1, b_xraw1
            cp("pool", xr[:, :, 0:3], xr[:, :, 512:515], [bxr], [bxr])
            for m in range(6):
                pi = m % 2
                for kd in range(KD):
                    mm(PS[pi], W1[:, kd, m * 128:(m + 1) * 128], hb[:, kd, :], kd == 0, kd == KD - 1,
                       [b_W1, b_h], [PSB[pi]])
                cp("act", xr[:, m, 3:515], PS[pi], [PSB[pi]], [bxr])
                ca, bca = cacc[m % 2], b_cacc[m % 2]
                act(ca, xr[:, m, 0:512], AF.Identity, [bxr, b_sm], [bca], bias=cvb[:, m:m + 1],
                    scale=cvw[:, m, 0:1])
                for k in range(1, 4):
                    stt("dve", ca, xr[:, m, k:k + 512], cvw[:, m, k:k + 1], ca,
                        ALU.mult, ALU.add, [bxr, b_sm, bca], [bca])
                act(xc[:, m, :], ca, AF.Silu, [bca], [b_xc[m]])
            for sub in range(4):
                pi = 2 + sub % 2
                for kd in range(KD):
                    mm(PS[pi], hb[:, kd, sub * 128:(sub + 1) * 128], W1[:, kd, 768:1280], kd == 0, kd == KD - 1,
                       [b_W1, b_h], [PSB[pi]])
                act(zs[:, sub, :], PS[pi], AF.Silu, [PSB[pi]], [b_zs[sub]])
            if t + 1 < NT1:
                load_norm(t + 1, xb, b_xb, hb2[(t + 1) % 2], b_h2[(t + 1) % 2], sq, b_sq, rstd, b_rstd, 7)
            dtr = dtraw[:, 4 * t:4 * t + 4, :].rearrange("p a b -> p (a b)")
            tt("dve", dt_t, dtr, dtb4, ALU.add, [b_dtraw[t], b_bc], [b_dt])
            act(dt_t, dt_t, AF.Exp, [b_dt], [b_dt])
            act(dt_t, dt_t, AF.Ln, [b_dt], [b_dt], bias=1.0)
            tt("dve", dtA, dt_t, A4, ALU.mult, [b_dt, b_A4], [b_dtA])
            dA3 = dtA.rearrange("p (a b) -> p a b", b=8)
            cp("dve", dAh[:, :, 0:8], dA3, [b_dtA], [b_dAh])
            tt("dve", dAtmp.rearrange("p (a b) -> p a b", b=8), dA3, dAh[:, :, 0:8], ALU.subtract,
               [b_dtA, b_dAh], [b_dAtmp])
            cp("dve", dAh[:, :, 8:16], dAtmp.rearrange("p (a b) -> p a b", b=8), [b_dAtmp], [b_dAh])
            for sub in range(4):
                pi = 4 + sub % 2
                pb = PSBF[pi]
                for m in range(4):
                    tr(pb[:, m * 128:(m + 1) * 128], xc[:, m, sub * 128:(sub + 1) * 128], ident_b,
                       [b_xc[m], b_cst], [PSB[pi]])
                tr(pb[:, 512:640], xc[:, 4, sub * 128:(sub + 1) * 128], ident_b, [b_xc[4], b_cst], [PSB[pi]])
                cp("act", xs_tok[:, sub, :], pb[:, 0:512], [PSB[pi]], [b_xs[sub]])
                cp("act", Btok[:, sub, :], pb[:, 512:640], [PSB[pi]], [b_Btok[sub]])
                tt("dve", Xt[:, sub, :].rearrange("p (a b) -> p a b", b=64),
                   xs_tok[:, sub, :].rearrange("p (a b) -> p a b", b=64),
                   bc(dt_t[:, sub * 8:(sub + 1) * 8].unsqueeze(2), [128, 8, 64]), ALU.mult,
                   [b_xs[sub], b_dt], [b_X[sub]])
            for c in range(2):
                ch = 2 * t + c
                j0, j1 = 2 * c, 2 * c + 1
                par = ch % 2
                dA0 = dtA[:, j0 * 8:(j0 + 1) * 8]
                dA1 = dtA[:, j1 * 8:(j1 + 1) * 8]
                h0, h1 = dAh[:, j0, :], dAh[:, j1, :]
                mm(PS[6][:, 0:16], tri_b, h0, True, True, [b_cst, b_dAh], [PSB[6]])
                mm(PS[6][:, 16:32], ones_b, h0, True, False, [b_cst, b_dAh], [PSB[6]])
                mm(PS[6][:, 16:32], tri_b, h1, False, True, [b_cst, b_dAh], [PSB[6]])
                mm(PS[6][:, 32:48], ones_b, h0, True, False, [b_cst, b_dAh], [PSB[6]])
                mm(PS[6][:, 32:48], ones_b, h1, False, True, [b_cst, b_dAh], [PSB[6]])
                cp("act", acs48, PS[6][:, 0:48], [PSB[6]], [b_acs48])
                a48 = acs48.rearrange("p (a b) -> p a b", b=16)
                tt("dve", acs, a48[:, :, 0:8], a48[:, :, 8:16], ALU.add, [b_acs48], [b_acs])
                act(ea, acs[:, 0:2, :], AF.Exp, [b_acs], [b_ea])
                tt("dve", dsx, bc(acs[:, 2:3, :], [128, 2, 8]), acs[:, 0:2, :], ALU.subtract, [b_acs], [b_dsx])
                act(dsx, dsx, AF.Exp, [b_dsx], [b_dsx])
                act(cdx, acs[:, 2, :], AF.Exp, [b_acs], [b_cdx])
                la0, la1, lb1 = LA[par * 2], LA[par * 2 + 1], LB[par]
                bla0, bla1, blb1 = b_LA[par * 2], b_LA[par * 2 + 1], b_LB[par]
                tt("dve", la0, bc(strict_b.unsqueeze(1), [128, 8, 128]), bc(dA0.unsqueeze(2), [128, 8, 128]),
                   ALU.mult, [b_cst, b_dtA], [bla0])
                tt("dve", la1, bc(strict_b.unsqueeze(1), [128, 8, 128]), bc(dA1.unsqueeze(2), [128, 8, 128]),
                   ALU.mult, [b_cst, b_dtA], [bla1])
                cp("pool", lb1, bc(dA1.unsqueeze(2), [128, 8, 128]), [b_dtA], [blb1])
                cc0 = c * 256
                BT = xc[:, 4, :]
                CT = xc[:, 5, :]
                mm(PS[7][:, 0:256], BT[:, cc0:cc0 + 128], CT[:, cc0:cc0 + 256], True, True, [b_xc[4], b_xc[5]],
                   [PSB[7]])
                mm(PS[7][:, 256:384], BT[:, cc0 + 128:cc0 + 256], CT[:, cc0 + 128:cc0 + 256], True, True,
                   [b_xc[4], b_xc[5]], [PSB[7]])
                tt("dve", cbm[par], PS[7][:, 0:384].rearrange("p (a b) -> p a b", b=128), causal3, ALU.mult,
                   [PSB[7], b_cst], [b_cbm[par]])
                def segmm(r):
                    pi = r % 2
                    mm(PS[pi][:, 0:128], la0[:, r, :], tri_b, True, True, [bla0, b_cst], [PSB[pi]])
                    mm(PS[pi][:, 128:256], la0[:, r, :], ones_b, True, False, [bla0, b_cst], [PSB[pi]])
                    mm(PS[pi][:, 128:256], lb1[:, r, :], tri_b, False, True, [blb1, b_cst], [PSB[pi]])
                    mm(PS[pi][:, 256:384], la1[:, r, :], tri_b, True, True, [bla1, b_cst], [PSB[pi]])

                def fin(r):
                    pi = r % 2
                    hp = r % 2
                    act(dec[hp], PS[pi][:, 0:384].rearrange("p (a b) -> p a b", b=128), AF.Exp, [PSB[pi]],
                        [b_dec[hp]])
                    tt("dve", MT[hp], dec[hp], cbm[par], ALU.mult, [b_dec[hp], b_cbm[par]], [b_MT[hp]])
                    cs = slice(r * 64, (r + 1) * 64)
                    mm(PS[2][:, cs], MT[hp][:, 0, :], Xt[:, j0, cs], True, True, [b_MT[hp], b_X[j0]], [PSB[2]])
                    mm(PS[3][:, cs], MT[hp][:, 1, :], Xt[:, j0, cs], True, False, [b_MT[hp], b_X[j0]], [PSB[3]])
                    mm(PS[3][:, cs], MT[hp][:, 2, :], Xt[:, j1, cs], False, True, [b_MT[hp], b_X[j1]], [PSB[3]])

                segmm(0)
                for r in range(8):
                    if r + 1 < 8:
                        segmm(r + 1)
                    fin(r)
                for lt, j in ((0, j0), (1, j1)):
                    mm(PS[4 + lt], CT[:, cc0 + lt * 128:cc0 + (lt + 1) * 128], hTb, True, True, [b_xc[5], b_hTb],
                       [PSB[4 + lt]])
                for lt, j in ((0, j0), (1, j1)):
                    tt("dve", Xd[:, j, :].rearrange("p (a b) -> p a b", b=64),
                       Xt[:, j, :].rearrange("p (a b) -> p a b", b=64),
                       bc(dsx[:, lt, :].unsqueeze(2), [128, 8, 64]), ALU.mult, [b_X[j], b_dsx], [b_Xd[j]])
                def comb_steps(lt, j):
                    tb, btb = t1[lt], b_t1[lt]
                    y, by = yb[lt], b_y[lt]
                    pb = PSBF[6 + lt]
                    return [
                        lambda: tt("dve", tb.rearrange("p (a b) -> p a b", b=64),
                                   PS[4 + lt].rearrange("p (a b) -> p a b", b=64),
                                   bc(ea[:, lt, :].unsqueeze(2), [128, 8, 64]), ALU.mult, [PSB[4 + lt], b_ea], [btb]),
                        lambda: tt("dve", tb, PS[2 + lt], tb, ALU.add, [PSB[2 + lt], btb], [btb]),
                        lambda: tt("pool", y.rearrange("p (a b) -> p a b", b=64),
                                   xs_tok[:, j, :].rearrange("p (a b) -> p a b", b=64), Dt, ALU.mult,
                                   [b_xs[j], b_Dt], [by]),
                        lambda: tt("dve", y, y, tb, ALU.add, [by, btb], [by]),
                        lambda: tt("dve", y, y, zs[:, j, :], ALU.mult, [by, b_zs[j]], [by]),
                        lambda: P.op("dve", lambda e: e.memset(ssq[:, lt:lt + 1], 0.0), writes=[b_ssqL[lt]]),
                        lambda: act(tb, y, AF.Square, [by], [btb, b_ssqL[lt]], accum=ssq[:, lt:lt + 1]),
                        lambda: ts("dve", ssq[:, 2 + lt:3 + lt], ssq[:, lt:lt + 1], 1.0 / 512, EPS, ALU.mult, ALU.add,
                                   [b_ssqL[lt]], [b_ssqL[lt]]),
                        lambda: act(ssq[:, 2 + lt:3 + lt], ssq[:, 2 + lt:3 + lt], AF.Sqrt, [b_ssqL[lt]], [b_ssqL[lt]]),
                        lambda: recip(ssq[:, 2 + lt:3 + lt], ssq[:, 2 + lt:3 + lt], [b_ssqL[lt]], [b_ssqL[lt]]),
                        lambda: stt("dve", yg[lt], y, ssq[:, 2 + lt:3 + lt], normw, ALU.mult, ALU.mult,
                                    [by, b_ssqL[lt], b_bc], [b_yg[lt]]),
                        lambda: [tr(pb[:, m * 128:(m + 1) * 128], yg[lt][:, m * 128:(m + 1) * 128], ident_b,
                                    [b_yg[lt], b_cst], [PSB[6 + lt]]) for m in range(4)],
                        lambda: cp("act", ygT[t % 2][:, :, j * 128:(j + 1) * 128],
                                   pb[:, 0:512].rearrange("p (a b) -> p a b", b=128), [PSB[6 + lt]], [b_ygT[t % 2]]),
                    ]
                st0, st1 = comb_steps(0, j0), comb_steps(1, j1)
                for f0, f1 in zip(st0, st1):
                    f0()
                    f1()
                mm(PS[0], Btok[:, j0, :], Xd[:, j0, :], True, False, [b_Btok[j0], b_Xd[j0]], [PSB[0]])
                mm(PS[0], Btok[:, j1, :], Xd[:, j1, :], False, True, [b_Btok[j1], b_Xd[j1]], [PSB[0]])
                tt("dve", hT.rearrange("p (a b) -> p a b", b=64), hT.rearrange("p (a b) -> p a b", b=64),
                   bc(cdx.unsqueeze(2), [128, 8, 64]), ALU.mult, [b_hT, b_cdx], [b_hT])
                tt("dve", hT, hT, PS[0], ALU.add, [b_hT, PSB[0]], [b_hT])
                cp("act", hTb, hT, [b_hT], [b_hTb])
            seg = (t * 512) // NTOK
            c0 = t * 512 - seg * NTOK
            P.dma("sp", ssd_in[seg].rearrange("(m p) s -> p m s", p=128)[:, :, c0:c0 + 512], ygT[t % 2],
                  reads=[b_ygT[t % 2]], writes=[b_ssdin[seg]])
            if (t + 1) % TPS == 0:
                P.dma("sp", ssd_tin[seg * 512:(seg + 1) * 512, :].rearrange("(m p) s -> p m s", p=128),
                      ygT[t % 2][:, :, 510:512], reads=[b_ygT[t % 2]], writes=[b_ssdtin])
                P.coll(lambda e, seg=seg: e.collective_compute(
                    "AllGather", ALU.bypass, replica_groups=[list(range(NCORE))],
                    ins=[ssd_in[seg]], outs=[ssd_out[seg]]), reads=[b_ssdin[seg]], writes=[b_ssdout[seg]])
        P.coll(lambda e: e.collective_compute(
            "AllGather", ALU.bypass, replica_groups=[list(range(NCORE))],
            ins=[ssd_tin], outs=[ssd_tout]), reads=[b_ssdtin], writes=[b_ssdtout])
        P.barrier()
        A.release(base_mark)
        if STOP == 1:
            for sg in range(NCORE):
                P.dma("sp", dbg_ssd[sg], ssd_in[sg], reads=[b_ssdin[sg]])
                if not SKIP3:
                    P.dma("sp", dbg_att[sg], att_in[sg], reads=[b_attin[sg]])
            P.barrier()
            P.emit_all()
            return nc

        if dbg:
            for sg in range(NCORE):
                P.dma("sp", dbg_ssd[sg], ssd_in[sg], reads=[b_ssdin[sg]])
                P.dma("sp", dbg_att[sg], att_in[sg], reads=[b_attin[sg]])
        gi = A.alloc([96], I32)
        b_gi = Buf()
        P.dma("sp", gi, gidx, writes=[b_gi])
        all_out = [b_ssdout[s] for s in range(NCORE)] + [b_attout[s] for s in range(NCORE)]
        ssd_rows = ssd_out.rearrange("a b c -> (a b) c")
        att_rows = att_out.rearrange("a b c -> (a b) c")
        h1T = A.alloc([KD, NT4], BF16)
        b_h1 = Buf()
        rs4 = A.alloc([NT4], F32)
        b_rs4 = Buf()
        mA_mark = A.mark()
        mT = A.alloc([KD, NT4], BF16)
        b_mT = [Buf() for _ in range(KD)]
        p4_mark = A.mark()

        all_out = all_out + [b_ssdtout, b_atttout]

        def gather(dst, rows, col, halo_col, bdst, trows):
            def em_main(e, dst=dst, rows=rows, col=col):
                return e.indirect_dma_start(out=dst[:, 2:NT4], out_offset=None, in_=rows[:, :],
                                            in_offset=bass.IndirectOffsetOnAxis(ap=gi[:, col:col + 1], axis=0))

            def em_halo(e, dst=dst, rows=rows, halo_col=halo_col):
                return e.indirect_dma_start(out=dst[:, 0:2], out_offset=None, in_=trows[:, :],
                                            in_offset=bass.IndirectOffsetOnAxis(ap=gi[:, halo_col:halo_col + 1],
                                                                                axis=0))
            P.dma("pool", None, None, reads=all_out + [b_gi], writes=[bdst], emit=em_main)
            P.dma("pool", None, None, reads=all_out + [b_gi], writes=[bdst], emit=em_halo)

        x4v = x4.rearrange("(k p) s -> p k s", p=128)
        xq = A.alloc([KD, NT4], BF16)
        b_xq = Buf()
        for g in range(4):
            P.dma("pool", xq[:, 4 * g:4 * g + 4, :], x4v[:, 4 * g:4 * g + 4, :], writes=[b_xq])
        sq4 = [A.alloc([512], BF16) for _ in range(2)]
        b_sq4 = [Buf(), Buf()]
        for ti, (a0, a1) in enumerate(T4):
            w = a1 - a0
            for kd in range(KD):
                s = sq4[kd % 2]
                act(s[:, 0:w], xq[:, kd, a0:a1], AF.Square, [b_xq], [b_sq4[kd % 2]])
                mm(PS[ti][:, 0:w], ones_b, s[:, 0:w], kd == 0, kd == KD - 1, [b_cst, b_sq4[kd % 2]], [PSB[ti]])
            rsqrt_chain(rs4[:, a0:a1], PS[ti][:, 0:w], 1.0 / D, [PSB[ti]], [b_rs4], None)
        for kd in range(KD):
            stt("dve", h1T[:, kd, :], xq[:, kd, :], anw[:, kd:kd + 1], rs4, ALU.mult, ALU.mult,
                [b_xq, b_sm, b_rs4], [b_h1])

        P.barrier()
        A.release(p4_mark)
        ygA = A.alloc([32, NT4], BF16)
        b_ygA = [Buf() for _ in range(32)]
        for k in range(32):
            gather(ygA[:, k, :], ssd_rows, k, 48 + k, b_ygA[k], ssd_tout)
        WS = [A.alloc([32, 128], BF16) for _ in range(2)]
        b_WS = [Buf(), Buf()]
        WG = [A.alloc([KD, 128], BF16) for _ in range(2)]
        b_WG = [Buf(), Buf()]
        gsb = [A.alloc([512], F32) for _ in range(2)]
        b_gsb = [Buf(), Buf()]
        for mt in range(KD):
            pr = mt % 2
            P.dma("pool", WS[pr], wso[mt].rearrange("p (a b) -> p a b", b=128), writes=[b_WS[pr]])
            P.dma("pool", WG[pr], wg[mt].rearrange("p (a b) -> p a b", b=128), writes=[b_WG[pr]])
            for ti, (a0, a1) in enumerate(T4):
                w = a1 - a0
                py, pg = (2 * ti) % 8, (2 * ti + 1) % 8
                for k in range(32):
                    mm(PS[py][:, 0:w], WS[pr][:, k, :], ygA[:, k, a0:a1], k == 0, k == 31, [b_WS[pr], b_ygA[k]],
                       [PSB[py]])
                for k in range(KD):
                    mm(PS[pg][:, 0:w], WG[pr][:, k, :], h1T[:, k, a0:a1], k == 0, k == KD - 1, [b_WG[pr], b_h1],
                       [PSB[pg]])
                g_ = gsb[ti % 2]
                act(g_[:, 0:w], PS[pg][:, 0:w], AF.Sigmoid, [PSB[pg], b_sm], [b_gsb[ti % 2]],
                    bias=bgate[:, mt:mt + 1])
                tt("dve", mT[:, mt, a0:a1], PS[py][:, 0:w], g_[:, 0:w], ALU.mult, [PSB[py], b_gsb[ti % 2]],
                   [b_mT[mt]])
        P.barrier()
        A.release(p4_mark)
        atA = A.alloc([KD, NT4], BF16)
        b_atA = [Buf() for _ in range(KD)]
        for k in range(KD):
            gather(atA[:, k, :], att_rows, 32 + k, 80 + k, b_atA[k], att_tout)
        WA = [A.alloc([KD, 128], BF16) for _ in range(2)]
        b_WA = [Buf(), Buf()]
        WG = [A.alloc([KD, 128], BF16) for _ in range(2)]
        b_WG = [Buf(), Buf()]
        gsb = [A.alloc([512], F32) for _ in range(2)]
        b_gsb = [Buf(), Buf()]
        for mt in range(KD):
            pr = mt % 2
            P.dma("pool", WA[pr], wao[mt].rearrange("p (a b) -> p a b", b=128), writes=[b_WA[pr]])
            P.dma("pool", WG[pr], wg[16 + mt].rearrange("p (a b) -> p a b", b=128), writes=[b_WG[pr]])
            for ti, (a0, a1) in enumerate(T4):
                w = a1 - a0
                py, pg = (2 * ti) % 8, (2 * ti + 1) % 8
                for k in range(KD):
                    mm(PS[py][:, 0:w], WA[pr][:, k, :], atA[:, k, a0:a1], k == 0, k == KD - 1,
                       [b_WA[pr], b_atA[k]], [PSB[py]])
                for k in range(KD):
                    mm(PS[pg][:, 0:w], WG[pr][:, k, :], h1T[:, k, a0:a1], k == 0, k == KD - 1, [b_WG[pr], b_h1],
                       [PSB[pg]])
                g_ = gsb[ti % 2]
                act(g_[:, 0:w], PS[pg][:, 0:w], AF.Sigmoid, [PSB[pg], b_sm], [b_gsb[ti % 2]],
                    bias=bgate[:, 16 + mt:17 + mt])
                tt("dve", g_[:, 0:w], PS[py][:, 0:w], g_[:, 0:w], ALU.mult, [PSB[py], b_gsb[ti % 2]],
                   [b_gsb[ti % 2]])
                tt("dve", mT[:, mt, a0:a1], mT[:, mt, a0:a1], g_[:, 0:w], ALU.add, [b_mT[mt], b_gsb[ti % 2]],
                   [b_mT[mt]])
        P.barrier()
        A.release(p4_mark)
        xmT = A.alloc([KD, NT4], F32)
        b_xm = [Buf() for _ in range(KD)]
        h2T = h1T
        b_h2 = b_h1
        WO = [A.alloc([KD, 128], BF16) for _ in range(2)]
        b_WO = [Buf(), Buf()]
        xres = [A.alloc([NT4], F32) for _ in range(2)]
        b_xres = [Buf(), Buf()]
        x4m = x4.rearrange("(k p) s -> k p s", p=128)
        xmidv = xmid.rearrange("(k p) s -> k p s", p=128)
        b_xmid = Buf()
        for mt in range(KD):
            pr = mt % 2
            P.dma("pool", WO[pr], wo[mt].rearrange("p (a b) -> p a b", b=128), writes=[b_WO[pr]])
            P.dma("sp", xres[pr], x4m[mt], writes=[b_xres[pr]])
            for ti, (a0, a1) in enumerate(T4):
                w = a1 - a0
                py = (mt * len(T4) + ti) % 8
                for k in range(KD):
                    mm(PS[py][:, 0:w], WO[pr][:, k, :], mT[:, k, a0:a1], k == 0, k == KD - 1, [b_WO[pr], b_mT[k]],
                       [PSB[py]])
                tt("dve", xmT[:, mt, a0:a1], PS[py][:, 0:w], xres[pr][:, a0:a1], ALU.add,
                   [PSB[py], b_xres[pr]], [b_xm[mt]])
            P.dma("sp", xmidv[mt], xmT[:, mt, :], reads=[b_xm[mt]], writes=[b_xmid])
        sq4 = [A.alloc([512], BF16) for _ in range(2)]
        b_sq4 = [Buf(), Buf()]
        for ti, (a0, a1) in enumerate(T4):
            w = a1 - a0
            for kd in range(KD):
                s = sq4[kd % 2]
                act(s[:, 0:w], xmT[:, kd, a0:a1], AF.Square, [b_xm[kd]], [b_sq4[kd % 2]])
                mm(PS[ti][:, 0:w], ones_b, s[:, 0:w], kd == 0, kd == KD - 1, [b_cst, b_sq4[kd % 2]], [PSB[ti]])
            rsqrt_chain(rs4[:, a0:a1], PS[ti][:, 0:w], 1.0 / D, [PSB[ti]], [b_rs4], None)
        for kd in range(KD):
            stt("dve", h2T[:, kd, :], xmT[:, kd, :], fnw[:, kd:kd + 1], rs4, ALU.mult, ALU.mult,
                [b_xm[kd], b_sm, b_rs4], [b_h2])
        P.barrier()
        A.release(mA_mark)
        aT = A.alloc([44, NTOK], BF16)
        b_aT = [Buf() for _ in range(44)]
        mE_mark = A.mark()
        fcwt = A.alloc([88, 4], F32)
        b_fcw = Buf()
        P.dma("sp", fcwt, fcw.rearrange("p (a b) -> p a b", b=4), writes=[b_fcw])
        WU = [A.alloc([2, KD, 128], BF16) for _ in range(2)]
        b_WU = [Buf(), Buf()]
        ur = [A.alloc([NT4], F32) for _ in range(2)]
        b_ur = [Buf() for _ in range(2)]
        uc = [A.alloc([NTOK], F32) for _ in range(4)]
        b_uc = [Buf() for _ in range(4)]
        pcount = 0
        for ft in range(44):
            pr = ft % 2
            P.dma("pool", WU[pr][:, 0], wup[ft].rearrange("p (a b) -> p a b", b=128), writes=[b_WU[pr]])
            P.dma("pool", WU[pr][:, 1], wup[44 + ft].rearrange("p (a b) -> p a b", b=128), writes=[b_WU[pr]])
            for half in range(2):
                fch = half * 44 + ft
                u, bu = ur[half], b_ur[half]
                for ti, (a0, a1) in enumerate(T4):
                    w = a1 - a0
                    py = pcount % 8
                    pcount += 1
                    for k in range(KD):
                        mm(PS[py][:, 0:w], WU[pr][:, half, k, :], h2T[:, k, a0:a1], k == 0, k == KD - 1,
                           [b_WU[pr], b_h2], [PSB[py]])
                    cp("act", u[:, a0:a1], PS[py][:, 0:w], [PSB[py]], [bu])
                ts("dve", u[:, 0:2], u[:, 0:2], flag[:, 0:1], None, ALU.mult, None, [bu, b_sm], [bu])
                c_, bc_ = uc[pr * 2 + half], b_uc[pr * 2 + half]
                act(c_, u[:, 0:NTOK], AF.Identity, [bu, b_fcw, b_sm], [bc_], bias=fcb[:, fch:fch + 1],
                    scale=fcwt[:, fch, 0:1])
                stt("dve", c_, u[:, 1:NTOK + 1], fcwt[:, fch, 1:2], c_, ALU.mult, ALU.add, [bu, b_fcw, bc_], [bc_])
                stt("dve", c_, u[:, 2:NTOK + 2], fcwt[:, fch, 2:3], c_, ALU.mult, ALU.add, [bu, b_fcw, bc_], [bc_])
            cg, bcg = uc[pr * 2], b_uc[pr * 2]
            cu, bcu = uc[pr * 2 + 1], b_uc[pr * 2 + 1]
            act(cg, cg, AF.Silu, [bcg], [bcg])
            tt("dve", aT[:, ft, :], cg, cu, ALU.mult, [bcg, bcu], [b_aT[ft]])
        if dbg:
            P.dma("sp", dbg_xm, xmid, reads=[b_xmid])
        P.barrier()
        A.release(mE_mark)
        WD = [A.alloc([44, 128], BF16) for _ in range(2)]
        b_WD = [Buf(), Buf()]
        xm2 = [A.alloc([NTOK], F32) for _ in range(2)]
        b_xm2 = [Buf(), Buf()]
        ob = [A.alloc([NTOK], F32) for _ in range(2)]
        b_ob = [Buf(), Buf()]
        b_out = Buf()
        outv = outT.rearrange("(k p) s -> k p s", p=128)
        TO = [(a, min(a + 512, NTOK)) for a in range(0, NTOK, 512)]
        for mt in range(KD):
            pr = mt % 2
            P.dma("pool", WD[pr], wdn[mt].rearrange("p (a b) -> p a b", b=128), writes=[b_WD[pr]])
            P.dma("sp", xm2[pr], xmidv[mt][:, 2:NT4], reads=[b_xmid], writes=[b_xm2[pr]])
            for ti, (a0, a1) in enumerate(TO):
                w = a1 - a0
                py = pcount % 8
                pcount += 1
                for k in range(44):
                    mm(PS[py][:, 0:w], WD[pr][:, k, :], aT[:, k, a0:a1], k == 0, k == 43, [b_WD[pr], b_aT[k]],
                       [PSB[py]])
                tt("dve", ob[pr][:, a0:a1], PS[py][:, 0:w], xm2[pr][:, a0:a1], ALU.add, [PSB[py], b_xm2[pr]],
                   [b_ob[pr]])
            P.dma("sp", outv[mt], ob[pr], reads=[b_ob[pr]], writes=[b_out])
        P.barrier()
        P.emit_all()
    return nc


def _rel_bucket_np(n):
    n = np.maximum(n, 0)
    nf = np.maximum(n, 1).astype(np.float32)
    large = 16 + (np.log(nf / np.float32(16)) / np.float32(math.log(128 / 16)) * np.float32(16)).astype(np.int32)
    large = np.minimum(large, 31)
    return np.where(n < 16, n, large)


def _wlay(W, kt, mt):
    K, M = W.shape
    return np.ascontiguousarray(W.reshape(kt, 128, mt, 128).transpose(2, 1, 0, 3).reshape(mt, 128, kt * 128))


_NC_CACHE = {}


DBG = False
import os
STOP = int(os.environ.get("MK_STOP", "0"))
NOCC = os.environ.get("MK_NOCC", "0") == "1"
SKIP3 = os.environ.get("MK_SKIP3", "0") == "1"
SKIP2 = os.environ.get("MK_SKIP2", "0") == "1"
SKIP1 = os.environ.get("MK_SKIP1", "0") == "1"
DBG_OUT = {}


def _prep(x, attn_norm_w, w_in, b_gate, ssm_conv_w, ssm_conv_b, ssm_dt_bias, ssm_a_log, ssm_d, ssm_norm_w,
          q_norm_w, k_norm_w, rel_bias, w_ssm_out, w_attn_out, w_out, ffn_norm_w, w_up, ffn_conv_w, ffn_conv_b,
          w_down):
    f32 = np.float32
    x = np.asarray(x, f32)
    S = x.shape[1]
    NTOK = S // NCORE
    NT4 = NTOK + 2
    w_in = np.asarray(w_in, f32)[0]
    xTfull = np.ascontiguousarray(x[0].T)
    p = np.arange(128)
    cstm = np.zeros((128, 1792), f32)
    cstm[:, 0:128] = np.eye(128)
    cstm[:, 128:256] = 1.0
    cstm[:, 256:384] = (p[:, None] <= p[None, :])
    cstm[:, 384:512] = (p[:, None] > p[None, :])
    cstm[:, 512:640] = (p[None, :] >= p[:, None])
    cstm[:, 640:768] = 1.0
    cstm[:, 768:896] = (p[None, :] >= p[:, None])
    cstm[:, 896:928] = 0.0
    cstm[:, 928:960] = -BIG
    cstm[:, 960:992] = 1.0
    cstm[:, 992:1024] = 0.0
    eselm = np.zeros((32, 32, 128), f32)
    for n in range(32):
        eselm[n, n, :] = 1.0
    eselm = eselm.reshape(32, 32 * 128)
    rt = np.arange(5)
    c = np.arange(512)
    kpos = (rt[None, :, None] - 1) * 128 + p[:, None, None]
    dist = c[None, None, :] - kpos
    cmaskm = np.where(dist >= 0, 0.0, -BIG).astype(f32).reshape(128, 5 * 512)
    bidx = _rel_bucket_np(dist)
    rel_bias = np.asarray(rel_bias, f32)
    fcw_full = np.asarray(ffn_conv_w, f32)[0]
    fcwm = np.zeros((128, 88, 4), f32)
    fcwm[:, :, 0:3] = fcw_full.T.reshape(88, 128, 3).transpose(1, 0, 2)
    fcwm = np.ascontiguousarray(fcwm.reshape(128, 88 * 4))
    wgm = _wlay(w_in[:, 16448:20544], 16, 32)
    wsom = _wlay(np.asarray(w_ssm_out, f32)[0], 32, 16)
    waom = _wlay(np.asarray(w_attn_out, f32)[0], 16, 16)
    wom = _wlay(np.asarray(w_out, f32)[0], 16, 16)
    wupm = _wlay(np.asarray(w_up, f32)[0], 16, 88)
    wdnm = _wlay(np.asarray(w_down, f32)[0], 44, 16)
    cw = np.asarray(ssm_conv_w, f32)[0]
    cb = np.asarray(ssm_conv_b, f32)[0]
    in_maps = []
    for cidx in range(NCORE):
        zc = np.arange(512 * cidx, 512 * cidx + 512)
        xc_ = 4096 + np.arange(512 * cidx, 512 * cidx + 512)
        Bc = 4096 + 4096 + np.arange(128 * cidx, 128 * cidx + 128)
        Cc = 4096 + 4096 + 1024 + np.arange(128 * cidx, 128 * cidx + 128)
        dtc = 4096 + 6144 + np.arange(8 * cidx, 8 * cidx + 8)
        qc = 10304 + np.arange(256 * cidx, 256 * cidx + 256)
        kc = 10304 + 2048 + np.arange(256 * cidx, 256 * cidx + 256)
        vc = 10304 + 4096 + np.arange(256 * cidx, 256 * cidx + 256)
        cols1 = np.concatenate([xc_, Bc, Cc, zc])
        cols2 = np.concatenate([qc, kc, vc, dtc])
        w1m = np.ascontiguousarray(w_in[:, cols1].reshape(16, 128, 1280).transpose(1, 0, 2))
        w2m = np.ascontiguousarray(w_in[:, cols2].reshape(16, 128, 776).transpose(1, 0, 2))
        smm = np.zeros((128, 512), f32)
        smm[:, 0:16] = np.asarray(attn_norm_w, f32)[0].reshape(16, 128).T
        smm[:, 16:32] = np.asarray(ffn_norm_w, f32)[0].reshape(16, 128).T
        smm[:, 32] = np.asarray(q_norm_w, f32)[0]
        smm[:, 33] = np.asarray(k_norm_w, f32)[0]
        smm[:, 34] = rel_bias[31, 2 * cidx]
        smm[:, 35] = rel_bias[31, 2 * cidx + 1]
        smm[:, 36] = 0.0 if cidx == 0 else 1.0
        smm[:, 40:72] = np.asarray(b_gate, f32)[0].reshape(32, 128).T
        xbc_cols = np.concatenate([xc_, Bc, Cc]) - 4096
        smm[:, 72:96] = cw[:, xbc_cols].T.reshape(6, 128, 4).transpose(1, 0, 2).reshape(128, 24)
        smm[:, 96:102] = cb[xbc_cols].reshape(6, 128).T
        smm[:, 128:216] = np.asarray(ffn_conv_b, f32)[0].reshape(88, 128).T
        bcm = np.zeros((128, 1152), f32)
        hs = np.arange(8 * cidx, 8 * cidx + 8)
        bcm[:, 0:32] = np.tile(np.asarray(ssm_dt_bias, f32)[0][hs], 4)[None, :]
        bcm[:, 32:64] = np.tile(np.asarray(ssm_a_log, f32)[0][hs], 4)[None, :]
        bcm[:, 64:72] = np.asarray(ssm_d, f32)[0][hs][None, :]
        bcm[:, 128:640] = np.asarray(ssm_norm_w, f32)[0][zc][None, :]
        tbm = np.stack([rel_bias[:, 2 * cidx][bidx], rel_bias[:, 2 * cidx + 1][bidx]], axis=1)
        tbm = np.ascontiguousarray(tbm.reshape(128, 2 * 5 * 512).astype(f32))
        x4m = np.zeros((D, NT4), f32)
        lo = cidx * NTOK - 2
        if lo < 0:
            x4m[:, 2:] = xTfull[:, 0:NTOK]
        else:
            x4m[:] = xTfull[:, lo:lo + NT4]
        gim = np.zeros((128, 96), np.int32)
        prev = (cidx + NCORE - 1) % NCORE
        for k in range(32):
            r, m = k // 4, k % 4
            gim[:, k] = cidx * (NCORE * 512) + r * 512 + m * 128 + p
            gim[:, 48 + k] = r * (NCORE * 512) + prev * 512 + m * 128 + p
        for k in range(16):
            r, m = k // 2, k % 2
            gim[:, 32 + k] = cidx * (NCORE * 256) + r * 256 + m * 128 + p
            gim[:, 80 + k] = r * (NCORE * 256) + prev * 256 + m * 128 + p
        in_maps.append({
            "xT": xTfull, "x4": x4m, "w1": w1m, "w2": w2m, "sm": smm, "bcst": bcm, "cst": cstm, "esel": eselm,
            "cmask": cmaskm, "tbr": tbm, "fcw": fcwm, "gidx": gim, "wg": wgm, "wso": wsom, "wao": waom,
            "wo": wom, "wup": wupm, "wdn": wdnm,
        })
    if STOP != 0:
        for m in in_maps:
            for k in ("wg", "wso", "wao", "wo", "wup", "wdn"):
                m[k] = m[k][0:1]
    return in_maps, S


def kernel(**inputs):
    f32 = np.float32
    in_maps, S = _prep(**inputs)
    NTOK = S // NCORE
    if S not in _NC_CACHE:
        _NC_CACHE[S] = build(S, dbg=DBG)
    nc = _NC_CACHE[S]
    res = run_bass_kernel_spmd(nc, in_maps, core_ids=list(range(NCORE)))
    if DBG:
        DBG_OUT["res"] = res.results
    out = np.zeros((1, S, D), f32)
    for cidx in range(NCORE):
        out[0, cidx * NTOK:(cidx + 1) * NTOK, :] = np.asarray(res.results[cidx]["outT"]).T
    return out
```
